# Optimizing a Trainium2 kernel written in Bass

```python
import jax, jax.numpy as jnp
from jax import lax
import numpy as np

D_MODEL = 1024
BATCH = 8
SEQ = 2048
DEPTH = 1

NSA_HEADS = 8
NSA_KV_GROUPS = 2
NSA_HEAD_DIM = 64
CMP_BLOCK = 32
CMP_STRIDE = 16
CMP_HIDDEN = 128
SEL_BLOCK = 64
SEL_TOPK = 8
WINDOW = 512
Q_BLOCK = 128
GLA_HEADS = 4
GLA_DK = 64
GLA_DV = 128
GLA_CHUNK = 64
GLA_LOWRANK = 16
GLA_TAU = 16.0
D_FF = 2816
EPS = 1e-6
NEG_INF = -1e30
FORCE = 1e9

NSA_Q_W = NSA_HEADS * NSA_HEAD_DIM
NSA_KV_W = NSA_KV_GROUPS * NSA_HEAD_DIM
GLA_K_W = GLA_HEADS * GLA_DK
GLA_V_W = GLA_HEADS * GLA_DV
D_MIX = NSA_Q_W + GLA_V_W
IN_SPLITS = (NSA_Q_W, NSA_KV_W, NSA_KV_W, NSA_KV_W, NSA_KV_W, NSA_KV_W, NSA_KV_W, NSA_HEADS * 3, GLA_K_W, GLA_K_W, GLA_V_W, GLA_V_W, GLA_LOWRANK)
D_IN_PROJ = sum(IN_SPLITS)

kernel_name = 'nsa_gla_macaron_hybrid'


def rmsnorm(x, g):
    xf = x.astype(jnp.float32)
    y = xf * lax.rsqrt(jnp.mean(xf * xf, axis=-1, keepdims=True) + EPS)
    return (y * g.astype(jnp.float32)).astype(x.dtype)


def swiglu_ffn(x, w_gate, w_up, w_down):
    return (jax.nn.silu(x @ w_gate) * (x @ w_up)) @ w_down


def split_columns(y, sizes):
    offsets = [int(o) for o in np.cumsum(sizes)[:-1]]
    return jnp.split(y, offsets, axis=-1)


def compress_blocks(kv, pe, w1, w2):
    B, T, G, dh = kv.shape
    n_cmp = (T - CMP_BLOCK) // CMP_STRIDE + 1
    idx = jnp.arange(n_cmp)[:, None] * CMP_STRIDE + jnp.arange(CMP_BLOCK)[None, :]
    blocks = kv[:, idx] + pe[None, None, :, None, :]
    blocks = blocks.transpose(0, 1, 3, 2, 4).reshape(B, n_cmp, G, CMP_BLOCK * dh)
    return jax.nn.gelu(blocks @ w1) @ w2


def nsa_attention(q, k_cmp, v_cmp, k_slc, v_slc, k_win, v_win, gates, pe_k, w1_k, w2_k, pe_v, w1_v, w2_v):
    B, T, H, dh = q.shape
    G = NSA_KV_GROUPS
    R = H // G
    f32 = jnp.float32
    scale = dh ** -0.5
    slopes = jnp.exp2(-8.0 * jnp.arange(1, H + 1, dtype=f32) / H).reshape(G, R)

    kc = compress_blocks(k_cmp, pe_k, w1_k, w2_k)
    vc = compress_blocks(v_cmp, pe_v, w1_v, w2_v)
    n_cmp = kc.shape[1]
    cmp_start = jnp.arange(n_cmp, dtype=jnp.int32) * CMP_STRIDE
    cmp_end = cmp_start + CMP_BLOCK - 1
    cmp_center = cmp_start.astype(f32) + (CMP_BLOCK - 1) / 2.0

    n_sel_blocks = T // SEL_BLOCK
    top_n = min(SEL_TOPK, n_sel_blocks)
    sel_start = jnp.arange(n_sel_blocks, dtype=jnp.int32) * SEL_BLOCK
    overlap = ((cmp_start[:, None] < sel_start[None, :] + SEL_BLOCK)
               & (cmp_start[:, None] + CMP_BLOCK > sel_start[None, :])).astype(f32)
    k_blk = k_slc.reshape(B, n_sel_blocks, SEL_BLOCK, G, dh).transpose(0, 3, 1, 2, 4)
    v_blk = v_slc.reshape(B, n_sel_blocks, SEL_BLOCK, G, dh).transpose(0, 3, 1, 2, 4)

    k_pad = jnp.pad(k_win, ((0, 0), (WINDOW, 0), (0, 0), (0, 0)))
    v_pad = jnp.pad(v_win, ((0, 0), (WINDOW, 0), (0, 0), (0, 0)))

    n_qb = T // Q_BLOCK
    q_blocks = q.reshape(B, n_qb, Q_BLOCK, G, R, dh).transpose(1, 0, 2, 3, 4, 5)
    g_blocks = gates.reshape(B, n_qb, Q_BLOCK, G, R, 3).transpose(1, 0, 2, 3, 4, 5)
    b_ix = jnp.arange(B)[:, None, None, None]
    g_ix = jnp.arange(G)[None, :, None, None]
    win_off = jnp.arange(WINDOW + Q_BLOCK, dtype=jnp.int32) - WINDOW
    sel_off = jnp.arange(SEL_BLOCK, dtype=jnp.int32)
    j_blk = jnp.arange(n_sel_blocks, dtype=jnp.int32)

    def block_fn(args):
        qb, gb, blk = args
        t = blk * Q_BLOCK + jnp.arange(Q_BLOCK, dtype=jnp.int32)
        tf = t.astype(f32)
        qs = qb * scale

        s = jnp.einsum('bqgrd,bngd->bgrqn', qs, kc).astype(f32)
        s = s - slopes[:, :, None, None] * (tf[:, None] - cmp_center[None, :])
        valid_c = cmp_end[None, :] <= t[:, None]
        s = jnp.where(valid_c, s, NEG_INF)
        p_c = jax.nn.softmax(s, axis=-1) * valid_c
        o_cmp = jnp.einsum('bgrqn,bngd->bqgrd', p_c.astype(vc.dtype), vc)

        imp = jnp.einsum('bgrqn,ns->bgqs', p_c, overlap)
        cur = t // SEL_BLOCK
        forced = (j_blk[None, :] == 0) | (j_blk[None, :] == cur[:, None]) | (j_blk[None, :] == cur[:, None] - 1)
        causal_blk = j_blk[None, :] <= cur[:, None]
        imp = jnp.where(forced, FORCE, imp)
        imp = jnp.where(causal_blk, imp, -FORCE)
        _, sel = lax.top_k(imp, top_n)
        k_sel = k_blk[b_ix, g_ix, sel]
        v_sel = v_blk[b_ix, g_ix, sel]
        kpos = sel[..., None] * SEL_BLOCK + sel_off
        dist = t[None, None, :, None, None] - kpos
        s = jnp.einsum('bqgrd,bgqnkd->bgrqnk', qs, k_sel).astype(f32)
        s = s - slopes[None, :, :, None, None, None] * dist[:, :, None].astype(f32)
        s = jnp.where((dist >= 0)[:, :, None], s, NEG_INF)
        p_s = jax.nn.softmax(s.reshape(B, G, R, Q_BLOCK, top_n * SEL_BLOCK), axis=-1).reshape(s.shape)
        o_slc = jnp.einsum('bgrqnk,bgqnkd->bqgrd', p_s.astype(v_sel.dtype), v_sel)

        kw = lax.dynamic_slice_in_dim(k_pad, blk * Q_BLOCK, WINDOW + Q_BLOCK, axis=1)
        vw = lax.dynamic_slice_in_dim(v_pad, blk * Q_BLOCK, WINDOW + Q_BLOCK, axis=1)
        kpos_w = blk * Q_BLOCK + win_off
        dist_w = t[:, None] - kpos_w[None, :]
        mask_w = (dist_w >= 0) & (dist_w < WINDOW) & (kpos_w[None, :] >= 0)
        s = jnp.einsum('bqgrd,bkgd->bgrqk', qs, kw).astype(f32)
        s = s - slopes[:, :, None, None] * dist_w.astype(f32)
        s = jnp.where(mask_w, s, NEG_INF)
        p_w = jax.nn.softmax(s, axis=-1)
        o_win = jnp.einsum('bgrqk,bkgd->bqgrd', p_w.astype(vw.dtype), vw)

        o = gb[..., 0:1] * o_cmp + gb[..., 1:2] * o_slc + gb[..., 2:3] * o_win
        return o.reshape(B, Q_BLOCK, H * dh)

    out = lax.map(block_fn, (q_blocks, g_blocks, jnp.arange(n_qb, dtype=jnp.int32)))
    return out.transpose(1, 0, 2, 3).reshape(B, T, H * dh)


def gla_attention(q, k, v, log_a):
    B, T, Hh, dk = q.shape
    dv = v.shape[-1]
    C = GLA_CHUNK
    N = T // C
    f32 = jnp.float32

    def chunks(z):
        return z.astype(f32).reshape(B, N, C, Hh, z.shape[-1]).transpose(1, 0, 3, 2, 4)

    qc = chunks(q) * (dk ** -0.5)
    kc = chunks(k)
    vc = chunks(v)
    b = jnp.cumsum(chunks(log_a), axis=3)
    b_last = b[:, :, :, -1:]
    q_in = qc * jnp.exp(b)
    k_in = kc * jnp.exp(-b)
    k_st = kc * jnp.exp(b_last - b)
    tril = jnp.tril(jnp.ones((C, C), dtype=bool))
    A = jnp.where(tril, jnp.einsum('nbhid,nbhjd->nbhij', q_in, k_in), 0.0)
    o_intra = jnp.einsum('nbhij,nbhjv->nbhiv', A, vc)

    def step(S, xs):
        qi, ki, vi, bl = xs
        o = jnp.einsum('bhid,bhdv->bhiv', qi, S)
        S = S * jnp.exp(bl)[:, :, 0, :, None] + jnp.einsum('bhjd,bhjv->bhdv', ki, vi)
        return S, o

    S0 = jnp.zeros((B, Hh, dk, dv), f32)
    _, o_inter = lax.scan(step, S0, (q_in, k_st, vc, b_last))
    o = o_intra + o_inter
    return o.transpose(1, 0, 3, 2, 4).reshape(B, T, Hh, dv)


def hybrid_mixer(h, w_in, pe_k, w1_k, w2_k, pe_v, w1_v, w2_v, w_a2, b_a, gla_g, w_out):
    B, T, _ = h.shape
    proj = h @ w_in
    (q, kc, vc, ks, vs, kw, vw, g_nsa, gq, gk, gv, gr, ga) = split_columns(proj, IN_SPLITS)
    kv = lambda z: z.reshape(B, T, NSA_KV_GROUPS, NSA_HEAD_DIM)
    gates = jax.nn.sigmoid(g_nsa.astype(jnp.float32)).astype(h.dtype).reshape(B, T, NSA_HEADS, 3)
    o_nsa = nsa_attention(q.reshape(B, T, NSA_HEADS, NSA_HEAD_DIM), kv(kc), kv(vc), kv(ks), kv(vs), kv(kw), kv(vw),
                          gates, pe_k, w1_k, w2_k, pe_v, w1_v, w2_v)

    log_a = jax.nn.log_sigmoid((ga @ w_a2 + b_a).astype(jnp.float32)) / GLA_TAU
    o_gla = gla_attention(gq.reshape(B, T, GLA_HEADS, GLA_DK), gk.reshape(B, T, GLA_HEADS, GLA_DK),
                          gv.reshape(B, T, GLA_HEADS, GLA_DV), log_a.reshape(B, T, GLA_HEADS, GLA_DK))
    o_gla = rmsnorm(o_gla, gla_g) * jax.nn.silu(gr.astype(jnp.float32)).reshape(B, T, GLA_HEADS, GLA_DV)
    o_gla = o_gla.reshape(B, T, GLA_V_W).astype(h.dtype)

    return jnp.concatenate([o_nsa, o_gla], axis=-1) @ w_out


def setup_inputs(seed: int = 0) -> dict:
    key = jax.random.key(seed)
    ks = jax.random.split(key, 24)
    f32 = jnp.float32

    def nrm(k, shape, scale):
        return jax.random.normal(k, shape, f32) * scale

    def gain(k, n):
        return 1.0 + 0.02 * jax.random.normal(k, (DEPTH, n), f32)

    L = DEPTH
    return {
        'x': nrm(ks[0], (BATCH, SEQ, D_MODEL), 1.0),
        'ffn1_norm': gain(ks[1], D_MODEL),
        'ffn1_w_gate': nrm(ks[2], (L, D_MODEL, D_FF), D_MODEL ** -0.5),
        'ffn1_w_up': nrm(ks[3], (L, D_MODEL, D_FF), D_MODEL ** -0.5),
        'ffn1_w_down': nrm(ks[4], (L, D_FF, D_MODEL), D_FF ** -0.5),
        'mix_norm': gain(ks[5], D_MODEL),
        'w_in': nrm(ks[6], (L, D_MODEL, D_IN_PROJ), D_MODEL ** -0.5),
        'nsa_pe_k': nrm(ks[7], (L, CMP_BLOCK, NSA_HEAD_DIM), 0.02),
        'nsa_w1_k': nrm(ks[8], (L, CMP_BLOCK * NSA_HEAD_DIM, CMP_HIDDEN), (CMP_BLOCK * NSA_HEAD_DIM) ** -0.5),
        'nsa_w2_k': nrm(ks[9], (L, CMP_HIDDEN, NSA_HEAD_DIM), CMP_HIDDEN ** -0.5),
        'nsa_pe_v': nrm(ks[10], (L, CMP_BLOCK, NSA_HEAD_DIM), 0.02),
        'nsa_w1_v': nrm(ks[11], (L, CMP_BLOCK * NSA_HEAD_DIM, CMP_HIDDEN), (CMP_BLOCK * NSA_HEAD_DIM) ** -0.5),
        'nsa_w2_v': nrm(ks[12], (L, CMP_HIDDEN, NSA_HEAD_DIM), CMP_HIDDEN ** -0.5),
        'gla_w_a2': nrm(ks[13], (L, GLA_LOWRANK, GLA_K_W), GLA_LOWRANK ** -0.5),
        'gla_b_a': nrm(ks[14], (L, GLA_K_W), 0.1),
        'gla_norm': gain(ks[15], GLA_DV),
        'w_out': nrm(ks[16], (L, D_MIX, D_MODEL), D_MIX ** -0.5),
        'ffn2_norm': gain(ks[17], D_MODEL),
        'ffn2_w_gate': nrm(ks[18], (L, D_MODEL, D_FF), D_MODEL ** -0.5),
        'ffn2_w_up': nrm(ks[19], (L, D_MODEL, D_FF), D_MODEL ** -0.5),
        'ffn2_w_down': nrm(ks[20], (L, D_FF, D_MODEL), D_FF ** -0.5),
        'final_norm': 1.0 + 0.02 * jax.random.normal(ks[21], (D_MODEL,), f32),
    }


def reference(x, ffn1_norm, ffn1_w_gate, ffn1_w_up, ffn1_w_down, mix_norm, w_in,
              nsa_pe_k, nsa_w1_k, nsa_w2_k, nsa_pe_v, nsa_w1_v, nsa_w2_v,
              gla_w_a2, gla_b_a, gla_norm, w_out,
              ffn2_norm, ffn2_w_gate, ffn2_w_up, ffn2_w_down, final_norm):
    for l in range(DEPTH):
        x = x + 0.5 * swiglu_ffn(rmsnorm(x, ffn1_norm[l]), ffn1_w_gate[l], ffn1_w_up[l], ffn1_w_down[l])
        x = x + hybrid_mixer(rmsnorm(x, mix_norm[l]), w_in[l],
                             nsa_pe_k[l], nsa_w1_k[l], nsa_w2_k[l], nsa_pe_v[l], nsa_w1_v[l], nsa_w2_v[l],
                             gla_w_a2[l], gla_b_a[l], gla_norm[l], w_out[l])
        x = x + 0.5 * swiglu_ffn(rmsnorm(x, ffn2_norm[l]), ffn2_w_gate[l], ffn2_w_up[l], ffn2_w_down[l])
    return rmsnorm(x, final_norm)
```

```python
import numpy as np
from contextlib import ExitStack, contextmanager
import concourse.bass as bass
import concourse.mybir as mybir
from concourse.bass_utils import run_bass_kernel_spmd

F32 = mybir.dt.float32
BF16 = mybir.dt.bfloat16
AF = mybir.ActivationFunctionType
ALU = mybir.AluOpType

EPS = 1e-6
O_Q, O_KC, O_VC, O_KS, O_VS, O_KW, O_VW, O_G, O_GQ, O_GK, O_GV, O_GR, O_GA = (
    0, 512, 640, 768, 896, 1024, 1152, 1280, 1304, 1560, 1816, 2328, 2840)
D_IN = 2856
BIGNEG = 30000.0


class Prog:
    ENGS = ("pe", "act", "dve", "pool", "sp")

    def __init__(self, nc, n_dma_sems=24):
        self.nc = nc
        self.ops = []
        self.eng_obj = {"pe": nc.tensor, "act": nc.scalar, "dve": nc.vector, "pool": nc.gpsimd, "sp": nc.sync}
        self.n_dma_sems = n_dma_sems

    def op(self, eng, fn, reads=(), writes=(), dma=False):
        self.ops.append(dict(eng=eng, fn=fn, reads=tuple(reads), writes=tuple(writes), dma=dma, bar=False))

    def dma(self, eng, fn, reads=(), writes=()):
        self.op(eng, fn, reads, writes, dma=True)

    def barrier(self):
        self.ops.append(dict(bar=True, eng=None, dma=False))

    def emit(self, final_wait_engine="sp"):
        nc = self.nc
        ops = self.ops
        n = len(ops)
        last_writer, readers = {}, {}
        deps = [set() for _ in range(n)]
        last_on_eng = {}
        dmas_since_bar = []
        pending = {e: set() for e in self.ENGS}
        for i, o in enumerate(ops):
            if o["bar"]:
                bd = set(last_on_eng.values()) | set(dmas_since_bar)
                for e in self.ENGS:
                    pending[e] |= bd
                dmas_since_bar = []
                last_writer, readers = {}, {}
                continue
            d = set()
            for k in o["reads"]:
                if k in last_writer:
                    d.add(last_writer[k])
            for k in o["writes"]:
                if k in last_writer:
                    d.add(last_writer[k])
                for r in readers.get(k, ()):
                    d.add(r)
            d |= pending[o["eng"]]
            pending[o["eng"]] = set()
            d.discard(i)
            dd = set()
            for j in d:
                oj = ops[j]
                if (not oj["dma"]) and oj["eng"] == o["eng"] and o["eng"] in ("pe", "sp"):
                    continue
                dd.add(j)
            deps[i] = dd
            for k in o["reads"]:
                readers.setdefault(k, []).append(i)
            for k in o["writes"]:
                last_writer[k] = i
                readers[k] = []
            if o["dma"]:
                dmas_since_bar.append(i)
            else:
                last_on_eng[o["eng"]] = i
        final = [i for i, o in enumerate(ops) if (not o["bar"]) and o["dma"]
                 and any(isinstance(k, tuple) and k and k[0] == "OUT" for k in o["writes"])]
        signal = [False] * n
        for i in range(n):
            for j in deps[i]:
                signal[j] = True
        for i, o in enumerate(ops):
            if (not o["bar"]) and o["dma"]:
                signal[i] = True
        eng_sem = {e: nc.alloc_semaphore("sem_" + e) for e in ("pe", "act", "dve", "pool")}
        nd = self.n_dma_sems
        qbase = {"sp": 0, "pool": nd, "act": 2 * nd}
        dma_sems = [nc.alloc_semaphore("sem_dma%d" % i) for i in range(3 * nd)]
        dma_sem_val = [0] * (3 * nd)
        dma_rrq = {"sp": 0, "pool": 0, "act": 0}
        eng_count = {e: 0 for e in eng_sem}
        sigval = [None] * n
        dma_rr = 0
        waited = {}
        nwaits = 0

        def do_wait(eng, sem, semid, val):
            nonlocal nwaits
            key = (eng, semid)
            if waited.get(key, 0) >= val:
                return
            waited[key] = val
            nwaits += 1
            self.eng_obj[eng].wait_ge(sem, val)

        for i, o in enumerate(ops):
            if o["bar"]:
                continue
            eng = o["eng"]
            eo = self.eng_obj[eng]
            for j in sorted(deps[i]):
                sem, semid, val = sigval[j]
                do_wait(eng, sem, semid, val)
            s = None
            if o["dma"]:
                s = qbase[eng] + dma_rrq[eng]
                dma_rrq[eng] = (dma_rrq[eng] + 1) % nd
                if dma_sem_val[s] > 0:
                    do_wait(eng, dma_sems[s], ("dma", s), dma_sem_val[s])
            ins = o["fn"](eo)
            if signal[i]:
                if o["dma"]:
                    dma_sem_val[s] += 16
                    ins.then_inc(dma_sems[s], 16)
                    sigval[i] = (dma_sems[s], ("dma", s), dma_sem_val[s])
                else:
                    eng_count[eng] += 1
                    ins.then_inc(eng_sem[eng], 1)
                    sigval[i] = (eng_sem[eng], ("eng", eng), eng_count[eng])
        for j in final:
            sem, semid, val = sigval[j]
            do_wait(final_wait_engine, sem, semid, val)
        return dict(n_ops=n, eng_count=eng_count, n_signal=sum(signal), n_waits=nwaits)


def make_consts(T):
    NT = T // 128
    NC = (T - 32) // 16 + 1
    NS = T // 64
    c = {}
    c["ident"] = np.eye(128, dtype=np.float32)
    k = np.arange(128)[:, None]
    q = np.arange(128)[None, :]
    diag = (k <= q).astype(np.float32)
    edge = (k > q).astype(np.float32)
    c["m_diag"] = np.tile(diag, (1, 4))
    c["m_edge"] = np.tile(edge, (1, 4))
    n = np.arange(128)[:, None]
    cm = np.zeros((NT, 128, 128), np.float32)
    for qi in range(NT):
        t = qi * 128 + np.arange(128)[None, :]
        m = ((16 * n + 31) <= t).astype(np.float32)
        m[NC:, :] = 0.0
        cm[qi] = (m - 1.0) * 30000.0
    c["m_cmp"] = cm
    j = np.arange(32)[None, :]
    mul = np.zeros((NT, 128, 32), np.float32)
    add = np.zeros((NT, 128, 32), np.float32)
    for qi in range(NT):
        t = qi * 128 + np.arange(128)[:, None]
        cur = t // 64
        forced = (j == 0) | (j == cur) | (j == cur - 1)
        causal = j <= cur
        valid = j < NS
        mul[qi] = ((~forced) & causal & valid).astype(np.float32)
        a = np.where(forced, 1e9, 0.0)
        a = np.where(causal, a, -1e9)
        a = np.where(valid, a, -2e9)
        add[qi] = a
    c["imp_mul"] = mul
    c["imp_add"] = add
    ncs = np.arange(128)[:, None] * 16
    ss = np.arange(32)[None, :] * 64
    ov = ((ncs < ss + 64) & (ncs + 32 > ss)).astype(np.float32)
    ov[NC:, :] = 0.0
    ov[:, NS:] = 0.0
    c["ov"] = ov
    slopes = np.exp2(-8.0 * np.arange(1, 9) / 8.0).reshape(2, 4)
    qa = np.zeros((2, 4, NT, 4, 128), np.float32)
    t = (np.arange(NT)[:, None] * 128 + np.arange(128)[None, :])
    aq = (t // 64).astype(np.float64)
    bq = (t % 64).astype(np.float64)
    for g in range(2):
        for r in range(4):
            s = slopes[g, r]
            qa[g, 0, :, r, :] = 64.0 * s
            qa[g, 1, :, r, :] = s
            qa[g, 2, :, r, :] = -s * 64.0 * aq
            qa[g, 3, :, r, :] = -s * bq
    c["qaug"] = qa.reshape(2, 4, NT * 512)
    kp = np.arange(T)
    ka = np.stack([kp // 64, kp % 64, np.ones(T), np.ones(T)]).astype(np.float32)
    c["kaug"] = ka
    nn = np.arange(128)
    kc = np.stack([nn // 4, 16.0 * (nn % 4) + 15.5, np.ones(128), np.ones(128)]).astype(np.float32)
    c["kaug_cmp"] = kc
    e = (np.arange(32)[:, None] == (kp[None, :] // 64)).astype(np.float32)
    c["esel"] = e
    tt = np.arange(512)
    c["g_reset"] = np.tile((tt % 64 != 0).astype(np.float32)[None, :], (64, 1))
    par = ((tt // 64) % 2)
    c["g_par"] = np.stack([np.tile((par == 0).astype(np.float32)[None, :], (64, 1)),
                           np.tile((par == 1).astype(np.float32)[None, :], (64, 1))])
    jj = np.arange(128)[:, None]
    ii = np.arange(128)[None, :]
    c["g_bd"] = ((jj <= ii) & (jj // 64 == ii // 64)).astype(np.float32)
    return c


CONST_SHAPES = None


def build(nc, T, D, DFF):
    NT = T // 128
    NDC = D // 128
    NFB = DFF // 128
    TG = 512
    NTG = T // TG
    DW = min(512, D)
    NDH = D // DW
    NC = (T - 32) // 16 + 1
    NCH = T // 64
    GRP = 4
    NSLOT = 8

    def din(name, shape):
        return nc.dram_tensor(name, list(shape), F32, kind="ExternalInput").ap()

    x_d = din("x", [T, D])
    out_d = nc.dram_tensor("out", [T, D], F32, kind="ExternalOutput").ap()
    g_d = {k: din(k, [128, D]) for k in ("g_ffn1", "g_mix", "g_ffn2", "g_fin")}
    ffn_d = {}
    for f in (1, 2):
        ffn_d[f] = (din("wg%d" % f, [NFB, 128, NDC, 128]), din("wu%d" % f, [NFB, 128, NDC, 128]),
                    din("wd%d" % f, [DFF, D]))
    win_d = din("w_in", [128, NDC, D_IN])
    w1_d = {"k": din("w1k", [64, 32, 128]), "v": din("w1v", [64, 32, 128])}
    w2_d = {"k": din("w2k", [128, 64]), "v": din("w2v", [128, 64])}
    pe_d = {"k": din("pekT", [64, 32]), "v": din("pevT", [64, 32])}
    wa2_d = din("w_a2", [16, 256])
    ba_d = din("b_a", [64, 4])
    glag_d = din("gla_g", [128, 128])
    wout_d = din("w_out", [1024, D])
    cs = make_consts(T)
    c_d = {k: din("c_" + k, v.shape) for k, v in cs.items()}

    P = Prog(nc)
    es_top = ExitStack()

    @contextmanager
    def scope():
        with ExitStack() as es_:
            yield es_
            P.barrier()

    _uid = [0]

    def sb(es, name, shape, dt):
        _uid[0] += 1
        return es.enter_context(nc.sbuf_tensor("sb%d_%s" % (_uid[0], name), list(shape), dt)).ap()

    psA = [nc.alloc_psum_tensor("psA%d" % i, [128, 512], F32).ap() for i in range(6)]
    psB = [nc.alloc_psum_tensor("psB%d" % i, [128, 1024], BF16).ap() for i in range(2)]

    def PS(i):
        return ("ps", i)

    def PB(i):
        return ("pb", i)

    x_sb = sb(es_top, "x_sb", [128, NT, D], F32)
    ident = sb(es_top, "ident", [128, 128], BF16)
    stat = sb(es_top, "stat", [128, 2 * NT], F32)
    ss = stat[:, 0:NT]
    rstd = stat[:, NT:2 * NT]

    def XK(tt, dh=None):
        if dh is None:
            return [("x", tt, h) for h in range(NDH)]
        return [("x", tt, dh)]

    for tt in range(NT):
        P.dma("sp", lambda e, tt=tt: e.dma_start(out=x_sb[:, tt, :], in_=x_d[tt * 128:(tt + 1) * 128, :]),
              writes=XK(tt))
    P.dma("pool", lambda e: e.dma_start(out=ident[:], in_=c_d["ident"][:, :]), writes=["ident"])

    def norm_to_hT(es, hT, g_dram, tag):
        gbc = sb(es, "gbc" + tag, [128, D], F32)
        hb = [sb(es, "hb%d" % i + tag, [128, D], BF16) for i in range(2)]
        junk = sb(es, "junk" + tag, [128, D], BF16)
        P.dma("sp", lambda e: e.dma_start(out=gbc[:], in_=g_dram[:, :]), writes=["gbc"])
        for tt in range(NT):
            P.op("act", lambda e, tt=tt: e.activation(out=junk[:], in_=x_sb[:, tt, :], func=AF.Square,
                                                      accum_out=ss[:, tt:tt + 1]),
                 reads=XK(tt), writes=["junk", ("ss", tt)])
        P.op("act", lambda e: e.activation(out=rstd, in_=ss, func=AF.Sqrt, scale=1.0 / D, bias=EPS),
             reads=[("ss", t) for t in range(NT)], writes=["rstd_tmp"])
        P.op("dve", lambda e: e.reciprocal(out=rstd, in_=rstd), reads=["rstd_tmp"], writes=["rstd"])
        for tt in range(NT):
            b = tt % 2
            P.op("dve", lambda e, tt=tt, b=b: e.scalar_tensor_tensor(
                out=hb[b][:], in0=x_sb[:, tt, :], scalar=rstd[:, tt:tt + 1], in1=gbc[:], op0=ALU.mult, op1=ALU.mult),
                reads=XK(tt) + ["rstd", "gbc"], writes=[("hb", b)])

            def tr(e, tt=tt, b=b):
                ins = None
                for dc in range(NDC):
                    ins = e.transpose(out=psB[b][:, dc * 128:(dc + 1) * 128], in_=hb[b][:, dc * 128:(dc + 1) * 128],
                                      identity=ident[:])
                return ins
            P.op("pe", tr, reads=[("hb", b), "ident"], writes=[PB(b)])
            P.op("act", lambda e, tt=tt, b=b: e.activation(
                out=hT[:, :, tt * 128:(tt + 1) * 128],
                in_=psB[b][:, 0:NDC * 128].rearrange("p (c t) -> p c t", c=NDC), func=AF.Copy),
                reads=[PB(b)], writes=[("hT", tt)])

    def ffn_phase(f, gname):
        wg_d, wu_d, wd_d = ffn_d[f]
        with ExitStack() as es:
            hT = sb(es, "hT_f", [128, NDC, T], BF16)
            wg = [sb(es, "wg%d" % s, [128, NDC, 128], BF16) for s in range(NSLOT)]
            wu = [sb(es, "wu%d" % s, [128, NDC, 128], BF16) for s in range(NSLOT)]
            wd = [sb(es, "wd%d" % s, [128, D], BF16) for s in range(NSLOT)]
            aT = [[sb(es, "aT%d_%d" % (p, i), [128, T], BF16) for i in range(GRP)] for p in range(2)]
            tmp = [sb(es, "silu%d" % i, [128, TG], F32) for i in range(2)]

            def load_w(fb):
                s = fb % NSLOT
                P.dma("pool", lambda e: e.dma_start(out=wg[s][:], in_=wg_d[fb]), writes=[("wg", s)])
                P.dma("pool", lambda e: e.dma_start(out=wu[s][:], in_=wu_d[fb]), writes=[("wu", s)])
                P.dma("pool", lambda e: e.dma_start(out=wd[s][:], in_=wd_d[fb * 128:(fb + 1) * 128, :]),
                      writes=[("wd", s)])

            for fb in range(min(NSLOT, NFB)):
                load_w(fb)
            with scope() as es2:
                norm_to_hT(es2, hT, g_d[gname], "_f")
            cnt = 0
            dcnt = 0
            groups = [list(range(a, min(a + GRP, NFB))) for a in range(0, NFB, GRP)]
            for gi, grp in enumerate(groups):
                par = gi % 2
                for fbi, fb in enumerate(grp):
                    s = fb % NSLOT
                    for tg in range(NTG):
                        bg = (2 * cnt) % 4
                        bu = (2 * cnt + 1) % 4
                        tb = cnt % 2
                        cnt += 1
                        hk = [("hT", tg * 4 + i) for i in range(4)]

                        def mm(e, w, bank, tg=tg, s=s):
                            ins = None
                            for dc in range(NDC):
                                ins = e.matmul(psA[bank][:, :], lhsT=w[s][:, dc, :], rhs=hT[:, dc, tg * TG:(tg + 1) * TG],
                                               start=(dc == 0), stop=(dc == NDC - 1))
                            return ins
                        P.op("pe", lambda e, bg=bg, mm=mm: mm(e, wg, bg), reads=hk + [("wg", s)], writes=[PS(bg)])
                        P.op("pe", lambda e, bu=bu, mm=mm: mm(e, wu, bu), reads=hk + [("wu", s)], writes=[PS(bu)])
                        P.op("act", lambda e, bg=bg, tb=tb: e.activation(out=tmp[tb][:], in_=psA[bg][:, :], func=AF.Silu),
                             reads=[PS(bg)], writes=[("tmp", tb)])
                        P.op("dve", lambda e, bu=bu, tb=tb, par=par, fbi=fbi, tg=tg: e.tensor_tensor(
                            out=aT[par][fbi][:, tg * TG:(tg + 1) * TG], in0=tmp[tb][:], in1=psA[bu][:, :], op=ALU.mult),
                            reads=[("tmp", tb), PS(bu)], writes=[("aT", par, fbi, tg)])
                for tt in range(NT):
                    for dh in range(NDH):
                        bk = 4 + (dcnt % 2)
                        dcnt += 1

                        def dmm(e, tt=tt, dh=dh, bk=bk, grp=grp, par=par):
                            ins = None
                            for fbi, fb in enumerate(grp):
                                ins = e.matmul(psA[bk][:, 0:DW], lhsT=aT[par][fbi][:, tt * 128:(tt + 1) * 128],
                                               rhs=wd[fb % NSLOT][:, dh * DW:(dh + 1) * DW],
                                               start=(fbi == 0), stop=(fbi == len(grp) - 1))
                            return ins
                        P.op("pe", dmm, reads=[("aT", par, fbi, tt // 4) for fbi in range(len(grp))] +
                             [("wd", fb % NSLOT) for fb in grp], writes=[PS(bk)])
                        P.op("dve", lambda e, tt=tt, dh=dh, bk=bk: e.scalar_tensor_tensor(
                            out=x_sb[:, tt, dh * DW:(dh + 1) * DW], in0=psA[bk][:, 0:DW], scalar=0.5,
                            in1=x_sb[:, tt, dh * DW:(dh + 1) * DW], op0=ALU.mult, op1=ALU.add),
                            reads=[PS(bk)] + XK(tt, dh), writes=XK(tt, dh))
                for fb in grp:
                    if fb + NSLOT < NFB:
                        load_w(fb + NSLOT)
        P.barrier()

    def mixer_phase():
        with ExitStack() as es:
            hT = sb(es, "hT_m", [128, NDC, T], BF16)
            o_tok = sb(es, "o_tok", [128, NT, 1024], BF16)
            wfm = [sb(es, "wfm%d" % i, [128, NDC, 64], BF16) for i in range(2)]
            wtm = [sb(es, "wtm%d" % i, [128, NDC, 256], BF16) for i in range(2)]
            with scope() as es2:
                norm_to_hT(es2, hT, g_d["g_mix"], "_m")
            wr = {"n": 0, "w": 0, "t": 0, "s": 0, "p": 0}

            def nextbank():
                b = wr["n"] % 4
                wr["n"] += 1
                return b

            def proj_fm(col0, M, evac):
                s = wr["w"] % 2
                wr["w"] += 1
                P.dma("pool", lambda e: e.dma_start(out=wfm[s][:, :, 0:M], in_=win_d[:, :, col0:col0 + M]),
                      writes=[("wfm", s)])
                for tg in range(NTG):
                    bank = nextbank()

                    def mm(e, tg=tg, bank=bank):
                        ins = None
                        for dc in range(NDC):
                            ins = e.matmul(psA[bank][0:M, :], lhsT=wfm[s][:, dc, 0:M], rhs=hT[:, dc, tg * TG:(tg + 1) * TG],
                                           start=(dc == 0), stop=(dc == NDC - 1))
                        return ins
                    P.op("pe", mm, reads=[("hT", tg * 4 + i) for i in range(4)] + [("wfm", s)], writes=[PS(bank)])
                    evac(tg, bank)

            def load_tm(pieces):
                s = wr["t"] % 2
                wr["t"] += 1
                off = 0
                for (c0, n_) in pieces:
                    P.dma("pool", lambda e, c0=c0, n_=n_, off=off: e.dma_start(
                        out=wtm[s][:, :, off:off + n_], in_=win_d[:, :, c0:c0 + n_]), writes=[("wtm", s)])
                    off += n_
                return s, off

            def mm_tm(e, s, N, tt, bank):
                ins = None
                for dc in range(NDC):
                    ins = e.matmul(psA[bank][:, 0:N], lhsT=hT[:, dc, tt * 128:(tt + 1) * 128], rhs=wtm[s][:, dc, 0:N],
                                   start=(dc == 0), stop=(dc == NDC - 1))
                return ins

            with scope() as esg:
                gaT = sb(esg, "gaT", [16, T], BF16)
                wa2 = sb(esg, "wa2", [16, 256], BF16)
                negb = sb(esg, "negb", [64, 4], F32)
                glag = sb(esg, "glag", [128, 128], F32)
                g_reset = sb(esg, "g_reset", [64, 512], BF16)
                g_par = [sb(esg, "g_par%d" % i, [64, 512], BF16) for i in range(2)]
                g_bd = sb(esg, "g_bd", [128, 128], BF16)
                P.dma("pool", lambda e: e.dma_start(out=wa2[:], in_=wa2_d[:, :]), writes=["wa2"])
                P.dma("sp", lambda e: e.dma_start(out=negb[:], in_=ba_d[:, :]), writes=["negb0"])
                P.dma("sp", lambda e: e.dma_start(out=glag[:], in_=glag_d[:, :]), writes=["glag"])
                P.dma("pool", lambda e: e.dma_start(out=g_reset[:], in_=c_d["g_reset"][:, :]), writes=["g_reset"])
                for i in range(2):
                    P.dma("pool", lambda e, i=i: e.dma_start(out=g_par[i][:], in_=c_d["g_par"][i]), writes=[("g_par", i)])
                P.dma("pool", lambda e: e.dma_start(out=g_bd[:], in_=c_d["g_bd"][:, :]), writes=["g_bd"])
                P.op("dve", lambda e: e.tensor_scalar(out=negb[:], in0=negb[:], scalar1=-1.0, scalar2=None, op0=ALU.mult),
                     reads=["negb0"], writes=["negb"])
                proj_fm(O_GA, 16, lambda tg, bank: P.op(
                    "act", lambda e: e.activation(out=gaT[0:16, tg * TG:(tg + 1) * TG], in_=psA[bank][0:16, :], func=AF.Copy),
                    reads=[PS(bank)], writes=[("gaT", tg)]))
                def gla_head(h):
                    with scope() as esh:
                        qin = sb(esh, "qin", [64, T], BF16)
                        q0 = sb(esh, "q0", [64, T], BF16)
                        q1 = sb(esh, "q1", [64, T], BF16)
                        kin = sb(esh, "kin", [64, T], BF16)
                        la = [sb(esh, "la%d" % i, [64, TG], F32) for i in range(2)]
                        bT = [sb(esh, "bT%d" % i, [64, TG], F32) for i in range(2)]
                        Ep = sb(esh, "Ep", [64, T], F32)
                        Em = sb(esh, "Em", [64, T], F32)
                        Sf = [sb(esh, "Sf%d" % i, [64, 128], F32) for i in range(2)]
                        Sb = sb(esh, "Sb", [64, 4, 128], BF16)
                        tb_ = [sb(esh, "tb%d" % i, [64, 128], F32) for i in range(4)]
                        gv = [sb(esh, "gv%d" % i, [128, 128], BF16) for i in range(2)]
                        gsl = [sb(esh, "gsl%d" % i, [128, 128], F32) for i in range(2)]
                        kt = [sb(esh, "kt%d" % i, [128, 64], BF16) for i in range(2)]
                        atm = [sb(esh, "atm%d" % i, [128, 128], BF16) for i in range(2)]
                        oraw = [sb(esh, "oraw%d" % i, [128, 128], F32) for i in range(2)]
                        junkg = sb(esh, "junkg", [128, 128], BF16)
                        st2 = sb(esh, "st2", [128, 4], F32)
                        for tg in range(NTG):
                            bank = nextbank()
                            lb = tg % 2
                            tgs = slice(tg * TG, (tg + 1) * TG)
                            P.op("pe", lambda e, bank=bank, tgs=tgs: e.matmul(
                                psA[bank][0:64, :], lhsT=wa2[0:16, h * 64:(h + 1) * 64], rhs=gaT[0:16, tgs], start=True, stop=True),
                                reads=["wa2", ("gaT", tg)], writes=[PS(bank)])
                            P.op("act", lambda e, bank=bank, lb=lb: e.activation(
                                out=la[lb][:], in_=psA[bank][0:64, :], func=AF.Exp, scale=-1.0, bias=negb[:, h:h + 1]),
                                reads=[PS(bank), "negb"], writes=[("la", lb)])
                            P.op("act", lambda e, lb=lb: e.activation(out=la[lb][:], in_=la[lb][:], func=AF.Ln, bias=1.0),
                                 reads=[("la", lb)], writes=[("la", lb)])
                            P.op("dve", lambda e, lb=lb: e.tensor_tensor_scan(
                                out=bT[lb][:], data0=g_reset[:], data1=la[lb][:], initial=0.0, op0=ALU.mult, op1=ALU.add),
                                reads=[("la", lb), "g_reset"], writes=[("bT", lb)])
                            P.op("act", lambda e, lb=lb, tgs=tgs: e.activation(out=Ep[:, tgs], in_=bT[lb][:], func=AF.Exp,
                                                                               scale=-1.0 / 16.0),
                                 reads=[("bT", lb)], writes=[("Ep", tg)])
                            P.op("act", lambda e, lb=lb, tgs=tgs: e.activation(out=Em[:, tgs], in_=bT[lb][:], func=AF.Exp,
                                                                               scale=1.0 / 16.0),
                                 reads=[("bT", lb)], writes=[("Em", tg)])

                        def evq(tg, bank):
                            tgs = slice(tg * TG, (tg + 1) * TG)
                            P.op("dve", lambda e: e.scalar_tensor_tensor(
                                out=qin[:, tgs], in0=psA[bank][0:64, :], scalar=0.125, in1=Ep[:, tgs],
                                op0=ALU.mult, op1=ALU.mult), reads=[PS(bank), ("Ep", tg)], writes=[("qin", tg)])
                            P.op("pool", lambda e: e.tensor_tensor(out=q0[:, tgs], in0=qin[:, tgs], in1=g_par[0][:], op=ALU.mult),
                                 reads=[("qin", tg), ("g_par", 0)], writes=[("q0", tg)])
                            P.op("pool", lambda e: e.tensor_tensor(out=q1[:, tgs], in0=qin[:, tgs], in1=g_par[1][:], op=ALU.mult),
                                 reads=[("qin", tg), ("g_par", 1)], writes=[("q1", tg)])
                        proj_fm(O_GQ + h * 64, 64, evq)

                        def evk(tg, bank):
                            tgs = slice(tg * TG, (tg + 1) * TG)
                            P.op("dve", lambda e: e.scalar_tensor_tensor(
                                out=kin[:, tgs], in0=psA[bank][0:64, :], scalar=1.0, in1=Em[:, tgs],
                                op0=ALU.mult, op1=ALU.mult), reads=[PS(bank), ("Em", tg)], writes=[("kin", tg)])
                        proj_fm(O_GK + h * 64, 64, evk)
                        s_gvr, _ = load_tm([(O_GV + h * 128, 128), (O_GR + h * 128, 128)])
                        P.op("dve", lambda e: e.memset(Sf[0][:], 0.0), writes=[("Sf", 0)])
                        P.op("pool", lambda e: e.memset(Sb[:, 0, :], 0.0), writes=[("Sb", 0)])
                        for tt in range(NT):
                            b2 = tt % 2
                            tts = slice(tt * 128, (tt + 1) * 128)
                            tgk = tt // 4
                            bank = nextbank()
                            P.op("pe", lambda e, tt=tt, bank=bank: mm_tm(e, s_gvr, 256, tt, bank),
                                 reads=[("hT", tt), ("wtm", s_gvr)], writes=[PS(bank)])
                            P.op("act", lambda e, bank=bank, b2=b2: e.activation(out=gv[b2][:], in_=psA[bank][:, 0:128], func=AF.Copy),
                                 reads=[PS(bank)], writes=[("gv", b2)])
                            P.op("act", lambda e, bank=bank, b2=b2: e.activation(out=gsl[b2][:], in_=psA[bank][:, 128:256], func=AF.Silu),
                                 reads=[PS(bank)], writes=[("gsl", b2)])
                            P.op("pool", lambda e, b2=b2: e.tensor_tensor(out=gsl[b2][:], in0=gsl[b2][:], in1=glag[:], op=ALU.mult),
                                 reads=[("gsl", b2), "glag"], writes=[("gsl", b2)])
                            P.op("pe", lambda e, b2=b2, tts=tts: e.transpose(out=psB[b2][:, 0:64], in_=kin[0:64, tts],
                                                                             identity=ident[0:64, 0:64]),
                                 reads=[("kin", tgk), "ident"], writes=[PB(b2)])
                            P.op("act", lambda e, b2=b2: e.activation(out=kt[b2][:], in_=psB[b2][:, 0:64], func=AF.Copy),
                                 reads=[PB(b2)], writes=[("kt", b2)])
                            for half in range(2):
                                c = 2 * tt + half
                                r0 = 64 * half
                                bk2 = 4 + (c % 2)
                                P.op("pe", lambda e, b2=b2, r0=r0, bk2=bk2: e.matmul(
                                    psA[bk2][0:64, 0:128], lhsT=kt[b2][r0:r0 + 64, 0:64], rhs=gv[b2][r0:r0 + 64, :],
                                    start=True, stop=True), reads=[("kt", b2), ("gv", b2)], writes=[PS(bk2)])
                                P.op("act", lambda e, c=c, bk2=bk2: e.activation(
                                    out=tb_[c % 4][:], in_=psA[bk2][0:64, 0:128], func=AF.Copy, scale=Ep[:, 64 * c + 63:64 * c + 64]),
                                    reads=[PS(bk2), ("Ep", c // 8)], writes=[("tb", c % 4)])
                                P.op("dve", lambda e, c=c: e.scalar_tensor_tensor(
                                    out=Sf[(c + 1) % 2][:], in0=Sf[c % 2][:], scalar=Ep[:, 64 * c + 63:64 * c + 64], in1=tb_[c % 4][:],
                                    op0=ALU.mult, op1=ALU.add),
                                    reads=[("Sf", c % 2), ("tb", c % 4), ("Ep", c // 8)], writes=[("Sf", (c + 1) % 2)])
                                P.op("pool", lambda e, c=c: e.tensor_copy(out=Sb[:, (c + 1) % 4, :], in_=Sf[(c + 1) % 2][:]),
                                     reads=[("Sf", (c + 1) % 2)], writes=[("Sb", (c + 1) % 4)])
                            bankA = nextbank()
                            P.op("pe", lambda e, bankA=bankA, tts=tts: e.matmul(
                                psA[bankA][:, 0:128], lhsT=kin[0:64, tts], rhs=qin[0:64, tts], start=True, stop=True),
                                reads=[("kin", tgk), ("qin", tgk)], writes=[PS(bankA)])
                            P.op("dve", lambda e, bankA=bankA, b2=b2: e.tensor_tensor(
                                out=atm[b2][:], in0=psA[bankA][:, 0:128], in1=g_bd[:], op=ALU.mult),
                                reads=[PS(bankA), "g_bd"], writes=[("atm", b2)])
                            bankO = nextbank()

                            def omm(e, bankO=bankO, b2=b2, tts=tts, tt=tt):
                                e.matmul(psA[bankO][:, 0:128], lhsT=atm[b2][:], rhs=gv[b2][:], start=True, stop=False)
                                e.matmul(psA[bankO][:, 0:128], lhsT=q0[0:64, tts], rhs=Sb[0:64, (2 * tt) % 4, :],
                                         start=False, stop=False)
                                return e.matmul(psA[bankO][:, 0:128], lhsT=q1[0:64, tts], rhs=Sb[0:64, (2 * tt + 1) % 4, :],
                                                start=False, stop=True)
                            P.op("pe", omm, reads=[("atm", b2), ("gv", b2), ("q0", tgk), ("q1", tgk),
                                                   ("Sb", (2 * tt) % 4), ("Sb", (2 * tt + 1) % 4)], writes=[PS(bankO)])
                            P.op("act", lambda e, bankO=bankO, b2=b2: e.activation(out=oraw[b2][:], in_=psA[bankO][:, 0:128], func=AF.Copy),
                                 reads=[PS(bankO)], writes=[("oraw", b2)])
                            P.op("act", lambda e, bankO=bankO: e.activation(out=junkg[:], in_=psA[bankO][:, 0:128], func=AF.Square,
                                                                            accum_out=st2[:, 0:1]),
                                 reads=[PS(bankO)], writes=["junkg", "ssg"])
                            P.op("act", lambda e: e.activation(out=st2[:, 1:2], in_=st2[:, 0:1], func=AF.Sqrt, scale=1.0 / 128.0, bias=EPS),
                                 reads=["ssg"], writes=["sqg"])
                            P.op("dve", lambda e: e.reciprocal(out=st2[:, 2:3], in_=st2[:, 1:2]), reads=["sqg"], writes=["rsg"])
                            P.op("dve", lambda e, tt=tt, b2=b2: e.scalar_tensor_tensor(
                                out=o_tok[:, tt, 512 + h * 128:512 + (h + 1) * 128], in0=oraw[b2][:], scalar=st2[:, 2:3],
                                in1=gsl[b2][:], op0=ALU.mult, op1=ALU.mult),
                                reads=[("oraw", b2), "rsg", ("gsl", b2)], writes=[("otok", tt, 8 + h)])

                for h_ in range(4):
                    gla_head(h_)

            with scope() as esn:
                gates = sb(esn, "gates", [128, NT, 24], F32)
                m_diag = sb(esn, "m_diag", [128, 512], BF16)
                m_edge = sb(esn, "m_edge", [128, 512], BF16)
                m_cmp = sb(esn, "m_cmp", [128, NT, 128], BF16)
                imp_mul = sb(esn, "imp_mul", [128, NT, 32], F32)
                imp_add = sb(esn, "imp_add", [128, NT, 32], F32)
                Mw = [sb(esn, "Mw%d" % i, [128, 128], BF16) for i in range(2)]
                pT = [sb(esn, "pT%d" % i, [128, 512], BF16) for i in range(3)]
                sm = sb(esn, "sm", [128, 64], F32)
                dcs, rdc, rds, rdw = sm[:, 0:4], sm[:, 4:8], sm[:, 8:12], sm[:, 12:16]
                cfc, cfs, cfw = sm[:, 16:20], sm[:, 20:24], sm[:, 24:28]
                top8 = sm[:, 32:40]
                imp = sb(esn, "imp", [128, 32], F32)
                imp2 = sb(esn, "imp2", [128, 32], F32)
                otmp = sb(esn, "otmp", [128, 64], F32)
                sclamp = sb(esn, "sclamp", [128, 512], F32)
                oc_sb = sb(esn, "oc_sb", [128, 388], F32)
                os_sb = sb(esn, "os_sb", [128, 260], F32)
                ow_sb = sb(esn, "ow_sb", [128, 260], F32)
                qa = sb(esn, "qa", [100, NT * 512], BF16)
                ka_s = sb(esn, "ka_s", [100, T], BF16)
                ka_w = sb(esn, "ka_w", [100, T], BF16)
                ka_c = sb(esn, "ka_c", [100, 128], BF16)
                kcin = sb(esn, "kcin", [64, T], BF16)
                vcin = sb(esn, "vcin", [64, T], BF16)
                vaug_s = sb(esn, "vaug_s", [128, NT, 65], BF16)
                vaug_w = sb(esn, "vaug_w", [128, NT, 65], BF16)
                vc_aug = sb(esn, "vc_aug", [128, 97], BF16)
                w1 = sb(esn, "w1", [64, 32, 128], BF16)
                w2 = sb(esn, "w2", [128, 64], BF16)
                peT = sb(esn, "peT", [64, 32], BF16)
                zc = sb(esn, "zc", [128, 128], F32)
                uc = sb(esn, "uc", [128, 128], F32)
                gel = sb(esn, "gel", [128, 128], BF16)
                bsum = sb(esn, "bsum", [128, 1], F32)
                o_nsa = None
                KD = 4096
                P.dma("pool", lambda e: e.dma_start(out=m_diag[:], in_=c_d["m_diag"][:, :]), writes=["m_diag"])
                P.dma("pool", lambda e: e.dma_start(out=m_edge[:], in_=c_d["m_edge"][:, :]), writes=["m_edge"])
                for qi in range(NT):
                    P.dma("pool", lambda e, qi=qi: e.dma_start(out=m_cmp[:, qi, :], in_=c_d["m_cmp"][qi]),
                          writes=[("m_cmp", qi)])
                    P.dma("sp", lambda e, qi=qi: e.dma_start(out=imp_mul[:, qi, :], in_=c_d["imp_mul"][qi]), writes=[("imp_mul", qi)])
                    P.dma("sp", lambda e, qi=qi: e.dma_start(out=imp_add[:, qi, :], in_=c_d["imp_add"][qi]), writes=[("imp_add", qi)])
                P.dma("pool", lambda e: e.dma_start(out=ka_s[64:96, :], in_=c_d["esel"][:, :], max_dma_last_dim=KD), writes=["ka_s_c"])
                P.dma("pool", lambda e: e.dma_start(out=ka_s[96:100, :], in_=c_d["kaug"][:, :], max_dma_last_dim=KD), writes=["ka_s_a"])
                P.dma("pool", lambda e: e.dma_start(out=ka_w[96:100, :], in_=c_d["kaug"][:, :], max_dma_last_dim=KD), writes=["ka_w_a"])
                P.dma("pool", lambda e: e.dma_start(out=ka_c[96:100, :], in_=c_d["kaug_cmp"][:, :]), writes=["ka_c_a"])
                P.op("dve", lambda e: e.memset(ka_w[64:96, :], 0.0), writes=["ka_w_c"])
                P.op("dve", lambda e: e.memset(ka_c[64:96, :], 0.0), writes=["ka_c_c"])
                P.op("dve", lambda e: e.memset(qa[64:96, :], 0.0), writes=[("qasel", q_) for q_ in range(NT)])
                for i in range(2):
                    P.op("dve", lambda e, i=i: e.memset(Mw[i][:], 0.0), writes=[("Mw", i)])
                P.op("dve", lambda e: e.memset(vaug_s[:, :, 64:65], 1.0), writes=["vaug_s_1"])
                P.op("dve", lambda e: e.memset(vaug_w[:, :, 64:65], 1.0), writes=["vaug_w_1"])
                P.op("dve", lambda e: e.memset(vc_aug[:, 64:65], 1.0), writes=["vc_1"])
                P.dma("pool", lambda e: e.dma_start(out=vc_aug[:, 65:97], in_=c_d["ov"][:, :]), writes=["vc_ov"])

                def nextS():
                    b = wr["s"] % 3
                    wr["s"] += 1
                    return b

                def nextP():
                    b = wr["p"] % 3
                    wr["p"] += 1
                    return b

                for g in range(2):
                    P.dma("pool", lambda e, g=g: e.dma_start(out=qa[96:100, :], in_=c_d["qaug"][g], max_dma_last_dim=KD),
                          writes=["qa_a"])
                    for r in range(4):
                        def evq(tg, bank, r=r):
                            dst = qa[0:64, tg * 2048:(tg + 1) * 2048].rearrange("p (a r q) -> p a r q", a=4, r=4)[:, :, r, :]
                            src = psA[bank][0:64, :].rearrange("p (a q) -> p a q", a=4)
                            P.op("act", lambda e: e.activation(out=dst, in_=src, func=AF.Copy, scale=0.125),
                                 reads=[PS(bank)], writes=[("qa", tg * 4 + i, r) for i in range(4)])
                        proj_fm(O_Q + (4 * g + r) * 64, 64, evq)

                    def evcopy(dst_t, key):
                        def ev(tg, bank):
                            P.op("act", lambda e: e.activation(out=dst_t[0:64, tg * TG:(tg + 1) * TG], in_=psA[bank][0:64, :], func=AF.Copy),
                                 reads=[PS(bank)], writes=[(key, tg)])
                        return ev
                    proj_fm(O_KS + g * 64, 64, evcopy(ka_s, "ka_s"))
                    proj_fm(O_KW + g * 64, 64, evcopy(ka_w, "ka_w"))
                    proj_fm(O_KC + g * 64, 64, evcopy(kcin, "kcin"))
                    proj_fm(O_VC + g * 64, 64, evcopy(vcin, "vcin"))
                    if g == 0:
                        s_v, Nv = load_tm([(O_VS, 64), (O_VW, 64), (O_G, 24)])
                    else:
                        s_v, Nv = load_tm([(O_VS + 64, 64), (O_VW + 64, 64)])
                    for tt in range(NT):
                        bank = nextbank()
                        P.op("pe", lambda e, tt=tt, bank=bank, s_v=s_v, Nv=Nv: mm_tm(e, s_v, Nv, tt, bank),
                             reads=[("hT", tt), ("wtm", s_v)], writes=[PS(bank)])
                        P.op("act", lambda e, tt=tt, bank=bank: e.activation(out=vaug_s[:, tt, 0:64], in_=psA[bank][:, 0:64], func=AF.Copy),
                             reads=[PS(bank)], writes=[("vaug_s", tt)])
                        P.op("act", lambda e, tt=tt, bank=bank: e.activation(out=vaug_w[:, tt, 0:64], in_=psA[bank][:, 64:128], func=AF.Copy),
                             reads=[PS(bank)], writes=[("vaug_w", tt)])
                        if g == 0:
                            P.op("act", lambda e, tt=tt, bank=bank: e.activation(out=gates[:, tt, :], in_=psA[bank][:, 128:152], func=AF.Sigmoid),
                                 reads=[PS(bank)], writes=[("gates", tt)])

                    def compress(which, src, srckey, is_k):
                        P.dma("pool", lambda e: e.dma_start(out=w1[:], in_=w1_d[which][:, :, :], max_dma_last_dim=4096), writes=["w1"])
                        P.dma("pool", lambda e: e.dma_start(out=w2[:], in_=w2_d[which][:, :]), writes=["w2"])
                        P.dma("pool", lambda e: e.dma_start(out=peT[:], in_=pe_d[which][:, :]), writes=["peT"])
                        bank = nextbank()

                        def hm(e):
                            ins = None
                            for l in range(32):
                                ins = e.matmul(psA[bank][:, 0:NC], lhsT=w1[0:64, l, :], rhs=src[0:64, l:l + 16 * (NC - 1) + 1:16],
                                               start=(l == 0), stop=(l == 31))
                            return ins
                        P.op("pe", hm, reads=["w1"] + [(srckey, t_) for t_ in range(NTG)], writes=[PS(bank)])
                        bank2 = nextbank()

                        def bm(e):
                            ins = None
                            for l in range(32):
                                ins = e.matmul(psA[bank2][:, 0:1], lhsT=w1[0:64, l, :], rhs=peT[0:64, l:l + 1],
                                               start=(l == 0), stop=(l == 31))
                            return ins
                        P.op("pe", bm, reads=["w1", "peT"], writes=[PS(bank2)])
                        P.op("act", lambda e: e.activation(out=bsum[:], in_=psA[bank2][:, 0:1], func=AF.Copy),
                             reads=[PS(bank2)], writes=["bsum"])
                        P.op("act", lambda e: e.activation(out=zc[:, 0:NC], in_=psA[bank][:, 0:NC], func=AF.Identity, bias=bsum[:, 0:1]),
                             reads=[PS(bank), "bsum"], writes=["zc"])
                        P.op("dve", lambda e: e.tensor_tensor(out=uc[:, 0:NC], in0=zc[:, 0:NC], in1=zc[:, 0:NC], op=ALU.mult),
                             reads=["zc"], writes=["uc"])
                        P.op("dve", lambda e: e.tensor_scalar(out=uc[:, 0:NC], in0=uc[:, 0:NC], scalar1=0.044715, scalar2=1.0,
                                                              op0=ALU.mult, op1=ALU.add), reads=["uc"], writes=["uc"])
                        P.op("dve", lambda e: e.tensor_tensor(out=uc[:, 0:NC], in0=uc[:, 0:NC], in1=zc[:, 0:NC], op=ALU.mult),
                             reads=["uc", "zc"], writes=["uc"])
                        P.op("act", lambda e: e.activation(out=uc[:, 0:NC], in_=uc[:, 0:NC], func=AF.Sigmoid,
                                                           scale=2.0 * float(np.sqrt(2.0 / np.pi))), reads=["uc"], writes=["uc"])
                        P.op("dve", lambda e: e.tensor_tensor(out=gel[:, 0:NC], in0=uc[:, 0:NC], in1=zc[:, 0:NC], op=ALU.mult),
                             reads=["uc", "zc"], writes=["gel"])
                        bank3 = nextbank()
                        if is_k:
                            P.op("pe", lambda e: e.matmul(psA[bank3][0:64, 0:NC], lhsT=w2[:, 0:64], rhs=gel[:, 0:NC], start=True, stop=True),
                                 reads=["w2", "gel"], writes=[PS(bank3)])
                            P.op("act", lambda e: e.activation(out=ka_c[0:64, 0:NC], in_=psA[bank3][0:64, 0:NC], func=AF.Copy),
                                 reads=[PS(bank3)], writes=["ka_c"])
                        else:
                            P.op("pe", lambda e: e.matmul(psA[bank3][0:NC, 0:64], lhsT=gel[:, 0:NC], rhs=w2[:, 0:64], start=True, stop=True),
                                 reads=["w2", "gel"], writes=[PS(bank3)])
                            P.op("act", lambda e: e.activation(out=vc_aug[0:NC, 0:64], in_=psA[bank3][0:NC, 0:64], func=AF.Copy),
                                 reads=[PS(bank3)], writes=["vc_aug"])
                    compress("k", kcin, "kcin", True)
                    compress("v", vcin, "vcin", False)

                    KA_C = ["ka_c", "ka_c_a", "ka_c_c"]
                    VC = ["vc_aug", "vc_1", "vc_ov"]
                    LAG = 2
                    for qi in range(NT):
                        qsl = slice(qi * 512, (qi + 1) * 512)
                        QK = [("qa", qi, r) for r in range(4)] + ["qa_a"]
                        gq = qi % 2
                        kb0 = max(0, qi - 4)
                        tiles = []

                        def mk_tile(kind, kb, qi=qi, qsl=qsl, QK=QK, kb0=kb0):
                            bS = nextS()
                            pb = nextP()
                            t = {}
                            if kind == "c":
                                def st():
                                    P.op("pe", lambda e: e.matmul(psA[bS][0:NC, :], lhsT=ka_c[0:100, 0:NC], rhs=qa[0:100, qsl],
                                                                  start=True, stop=True), reads=KA_C + QK, writes=[PS(bS)])

                                def ex():
                                    P.op("dve", lambda e: e.scalar_tensor_tensor(
                                        out=sclamp[0:NC, :].rearrange("p (r q) -> p r q", r=4),
                                        in0=psA[bS][0:NC, :].rearrange("p (r q) -> p r q", r=4), scalar=50.0,
                                        in1=m_cmp[0:NC, qi, :].unsqueeze(1).to_broadcast([NC, 4, 128]),
                                        op0=ALU.min, op1=ALU.add), reads=[PS(bS), ("m_cmp", qi)], writes=["sclamp"])
                                    P.op("act", lambda e: e.activation(out=pT[pb][0:NC, :], in_=sclamp[0:NC, :], func=AF.Exp),
                                         reads=["sclamp"], writes=[("pT", pb)])

                                def pv():
                                    def f(e):
                                        ins = None
                                        for r in range(4):
                                            ins = e.matmul(psA[3][:, r * 97:(r + 1) * 97], lhsT=pT[pb][0:NC, r * 128:(r + 1) * 128],
                                                           rhs=vc_aug[0:NC, 0:97], start=True, stop=True)
                                        return ins
                                    P.op("pe", f, reads=[("pT", pb)] + VC, writes=[PS(3)])
                            else:
                                ka = ka_w if kind == "w" else ka_s
                                kkeys = ([("ka_w", kb // 4), "ka_w_a", "ka_w_c"] if kind == "w"
                                         else [("ka_s", kb // 4), "ka_s_a", "ka_s_c", ("qasel", qi)])
                                va = vaug_w if kind == "w" else vaug_s
                                vkeys = [("vaug_w", kb), "vaug_w_1"] if kind == "w" else [("vaug_s", kb), "vaug_s_1"]
                                obank = 5 if kind == "w" else 4
                                first = (kb == kb0) if kind == "w" else (kb == 0)
                                mask = None
                                if kb == qi:
                                    mask = (m_diag, "m_diag")
                                elif kind == "w" and kb == qi - 4:
                                    mask = (m_edge, "m_edge")

                                def st():
                                    P.op("pe", lambda e: e.matmul(psA[bS][:, :], lhsT=ka[0:100, kb * 128:(kb + 1) * 128], rhs=qa[0:100, qsl],
                                                                  start=True, stop=True), reads=kkeys + QK, writes=[PS(bS)])

                                def ex():
                                    P.op("act", lambda e: e.activation(out=pT[pb][:], in_=psA[bS][:, :], func=AF.Exp),
                                         reads=[PS(bS)], writes=[("pT", pb)])
                                    if mask is not None:
                                        P.op("dve", lambda e: e.tensor_tensor(out=pT[pb][:], in0=pT[pb][:], in1=mask[0][:], op=ALU.mult),
                                             reads=[("pT", pb), mask[1]], writes=[("pT", pb)])

                                def pv():
                                    def f(e):
                                        ins = None
                                        for r in range(4):
                                            ins = e.matmul(psA[obank][:, r * 65:(r + 1) * 65], lhsT=pT[pb][:, r * 128:(r + 1) * 128],
                                                           rhs=va[:, kb, 0:65], start=(first and r == 0), stop=(kb == qi),
                                                           skip_group_check=True)
                                        return ins
                                    P.op("pe", f, reads=[("pT", pb)] + vkeys, writes=[PS(obank)])
                            t["st"], t["ex"], t["pv"] = st, ex, pv
                            return t

                        def sel_dve(qi=qi, gq=gq):
                            P.op("act", lambda e: e.activation(out=oc_sb[:], in_=psA[3][:, 0:388], func=AF.Copy),
                                 reads=[PS(3)], writes=["oc_sb"])
                            P.op("dve", lambda e: e.tensor_scalar(out=dcs, in0=oc_sb[:, 64:388:97], scalar1=1e-30, scalar2=None, op0=ALU.max),
                                 reads=["oc_sb"], writes=["dcs"])
                            P.op("dve", lambda e: e.reciprocal(out=rdc, in_=dcs), reads=["dcs"], writes=["rdc"])
                            P.op("dve", lambda e: e.tensor_scalar(out=imp[:], in0=oc_sb[:, 65:97], scalar1=rdc[:, 0:1], scalar2=None, op0=ALU.mult),
                                 reads=["oc_sb", "rdc"], writes=["imp"])
                            for r in range(1, 4):
                                P.op("dve", lambda e, r=r: e.scalar_tensor_tensor(
                                    out=imp[:], in0=oc_sb[:, r * 97 + 65:r * 97 + 97], scalar=rdc[:, r:r + 1], in1=imp[:],
                                    op0=ALU.mult, op1=ALU.add), reads=["oc_sb", "rdc", "imp"], writes=["imp"])
                            P.op("dve", lambda e: e.tensor_tensor(out=imp2[:], in0=imp[:], in1=imp_mul[:, qi, :], op=ALU.mult),
                                 reads=["imp", ("imp_mul", qi)], writes=["imp2"])
                            P.op("dve", lambda e: e.tensor_tensor(out=imp2[:], in0=imp2[:], in1=imp_add[:, qi, :], op=ALU.add),
                                 reads=["imp2", ("imp_add", qi)], writes=["imp2"])
                            P.op("dve", lambda e: e.max(out=top8, in_=imp2[:]), reads=["imp2"], writes=["top8"])
                            P.op("dve", lambda e: e.tensor_scalar(out=Mw[gq][:, 64:96], in0=imp2[:], scalar1=top8[:, 7:8], scalar2=-1.0,
                                                                  op0=ALU.is_ge, op1=ALU.add),
                                 reads=["imp2", "top8"], writes=[("Mw", gq)])

                        def sel_apply(qi=qi, gq=gq):
                            P.op("pe", lambda e: e.transpose(out=psB[gq][:, 0:128], in_=Mw[gq][:], identity=ident[:]),
                                 reads=[("Mw", gq), "ident"], writes=[PB(gq)])
                            P.op("act", lambda e: e.activation(
                                out=qa[64:96, qi * 512:(qi + 1) * 512].rearrange("p (r q) -> p r q", r=4),
                                in_=psB[gq][64:96, 0:128].unsqueeze(1).to_broadcast([32, 4, 128]),
                                func=AF.Copy, scale=BIGNEG), reads=[PB(gq)], writes=[("qasel", qi)])

                        tiles.append(mk_tile("c", 0))
                        tiles[-1]["post"] = sel_dve
                        for kb in range(kb0, qi + 1):
                            tiles.append(mk_tile("w", kb))
                        first_s = len(tiles)
                        for kb in range(0, qi + 1):
                            tiles.append(mk_tile("s", kb))
                        tiles[first_s]["pre"] = sel_apply
                        n_t = len(tiles)
                        for i in range(n_t + LAG):
                            j = i - LAG
                            if j >= 0:
                                tiles[j]["pv"]()
                                if "post" in tiles[j]:
                                    tiles[j]["post"]()
                            if i < n_t:
                                if "pre" in tiles[i]:
                                    tiles[i]["pre"]()
                                tiles[i]["st"]()
                                tiles[i]["ex"]()
                        P.op("act", lambda e: e.activation(out=os_sb[:], in_=psA[4][:, 0:260], func=AF.Copy), reads=[PS(4)], writes=["os_sb"])
                        P.op("act", lambda e: e.activation(out=ow_sb[:], in_=psA[5][:, 0:260], func=AF.Copy), reads=[PS(5)], writes=["ow_sb"])
                        P.op("dve", lambda e: e.reciprocal(out=rds, in_=os_sb[:, 64:260:65]), reads=["os_sb"], writes=["rds"])
                        P.op("dve", lambda e: e.reciprocal(out=rdw, in_=ow_sb[:, 64:260:65]), reads=["ow_sb"], writes=["rdw"])
                        for (cf, rd, x_, nm) in ((cfc, rdc, 0, "c"), (cfs, rds, 1, "s"), (cfw, rdw, 2, "w")):
                            P.op("dve", lambda e, cf=cf, rd=rd, x_=x_, qi=qi, g=g: e.tensor_tensor(
                                out=cf, in0=rd, in1=gates[:, qi, 12 * g + x_:12 * g + 12:3], op=ALU.mult),
                                reads=["rd" + nm, ("gates", qi)], writes=["cf" + nm])
                        for r in range(4):
                            P.op("dve", lambda e, r=r: e.tensor_scalar(out=otmp[:], in0=oc_sb[:, r * 97:r * 97 + 64], scalar1=cfc[:, r:r + 1],
                                                                       scalar2=None, op0=ALU.mult), reads=["oc_sb", "cfc"], writes=["otmp"])
                            P.op("dve", lambda e, r=r: e.scalar_tensor_tensor(out=otmp[:], in0=os_sb[:, r * 65:r * 65 + 64], scalar=cfs[:, r:r + 1],
                                                                              in1=otmp[:], op0=ALU.mult, op1=ALU.add),
                                 reads=["os_sb", "cfs", "otmp"], writes=["otmp"])
                            hh = 4 * g + r
                            P.op("dve", lambda e, r=r, hh=hh, qi=qi: e.scalar_tensor_tensor(
                                out=o_tok[:, qi, hh * 64:(hh + 1) * 64], in0=ow_sb[:, r * 65:r * 65 + 64], scalar=cfw[:, r:r + 1],
                                in1=otmp[:], op0=ALU.mult, op1=ALU.add),
                                reads=["ow_sb", "cfw", "otmp"], writes=[("otok", qi, hh)])

            with scope() as eso:
                wout = sb(eso, "wout", [128, 8, D], BF16)
                oT = [sb(eso, "oT%d" % i, [128, 8, 128], BF16) for i in range(2)]
                P.dma("pool", lambda e: e.dma_start(out=wout[:], in_=wout_d.rearrange("(c p) d -> p c d", p=128)), writes=["wout"])
                k = 0
                for tt in range(NT):
                    b2 = tt % 2

                    def tr(e, tt=tt, b2=b2):
                        ins = None
                        for c in range(8):
                            ins = e.transpose(out=psB[b2][:, c * 128:(c + 1) * 128], in_=o_tok[:, tt, c * 128:(c + 1) * 128],
                                              identity=ident[:])
                        return ins
                    P.op("pe", tr, reads=[("otok", tt, c) for c in range(12)] + ["ident"], writes=[PB(b2)])
                    P.op("act", lambda e, b2=b2: e.activation(out=oT[b2][:], in_=psB[b2][:, 0:1024].rearrange("p (c t) -> p c t", c=8),
                                                              func=AF.Copy), reads=[PB(b2)], writes=[("oT", b2)])
                    for dh in range(NDH):
                        bank = 4 + (k % 2)
                        k += 1

                        def mm(e, b2=b2, dh=dh, bank=bank):
                            ins = None
                            for c in range(8):
                                ins = e.matmul(psA[bank][:, 0:DW], lhsT=oT[b2][:, c, :], rhs=wout[:, c, dh * DW:(dh + 1) * DW],
                                               start=(c == 0), stop=(c == 7))
                            return ins
                        P.op("pe", mm, reads=[("oT", b2), "wout"], writes=[PS(bank)])
                        P.op("dve", lambda e, tt=tt, dh=dh, bank=bank: e.tensor_tensor(
                            out=x_sb[:, tt, dh * DW:(dh + 1) * DW], in0=psA[bank][:, 0:DW], in1=x_sb[:, tt, dh * DW:(dh + 1) * DW],
                            op=ALU.add), reads=[PS(bank)] + XK(tt, dh), writes=XK(tt, dh))
        P.barrier()

    def final_phase():
        with ExitStack() as es:
            gbc = sb(es, "gbc_o", [128, D], F32)
            ob = [sb(es, "ob%d" % i, [128, D], F32) for i in range(2)]
            junk = sb(es, "junk_o", [128, D], BF16)
            P.dma("sp", lambda e: e.dma_start(out=gbc[:], in_=g_d["g_fin"][:, :]), writes=["gbc"])
            for tt in range(NT):
                P.op("act", lambda e, tt=tt: e.activation(out=junk[:], in_=x_sb[:, tt, :], func=AF.Square, accum_out=ss[:, tt:tt + 1]),
                     reads=XK(tt), writes=["junk", ("ss", tt)])
            P.op("act", lambda e: e.activation(out=rstd, in_=ss, func=AF.Sqrt, scale=1.0 / D, bias=EPS),
                 reads=[("ss", t) for t in range(NT)], writes=["rstd_tmp"])
            P.op("dve", lambda e: e.reciprocal(out=rstd, in_=rstd), reads=["rstd_tmp"], writes=["rstd"])
            for tt in range(NT):
                b = tt % 2
                P.op("dve", lambda e, tt=tt, b=b: e.scalar_tensor_tensor(
                    out=ob[b][:], in0=x_sb[:, tt, :], scalar=rstd[:, tt:tt + 1], in1=gbc[:], op0=ALU.mult, op1=ALU.mult),
                    reads=XK(tt) + ["rstd", "gbc"], writes=[("ob", b)])
                P.dma("sp", lambda e, tt=tt, b=b: e.dma_start(out=out_d[tt * 128:(tt + 1) * 128, :], in_=ob[b][:]),
                      reads=[("ob", b)], writes=[("OUT", tt)])

    import os as _os
    stages = _os.environ.get("MK_STAGES", "f1,mix,f2")
    if "f1" in stages:
        ffn_phase(1, "g_ffn1")
    if "mix" in stages:
        mixer_phase()
    if "f2" in stages:
        ffn_phase(2, "g_ffn2")
    final_phase()
    stats = P.emit()
    es_top.close()
    return stats, cs


def prep_weights(inp, T, D, DFF):
    NDC = D // 128
    NFB = DFF // 128
    f32 = np.float32
    w = {}

    def bc(v):
        return np.ascontiguousarray(np.broadcast_to(np.asarray(v, f32).reshape(1, -1), (128, v.size)))
    w["g_ffn1"] = bc(inp["ffn1_norm"][0])
    w["g_mix"] = bc(inp["mix_norm"][0])
    w["g_ffn2"] = bc(inp["ffn2_norm"][0])
    w["g_fin"] = bc(inp["final_norm"])
    ffn_in = {1: (inp["ffn1_w_gate"], inp["ffn1_w_up"], inp["ffn1_w_down"]),
              2: (inp["ffn2_w_gate"], inp["ffn2_w_up"], inp["ffn2_w_down"])}
    for f in (1, 2):
        for nm, arr in (("wg", ffn_in[f][0]), ("wu", ffn_in[f][1])):
            a = np.asarray(arr[0], f32)
            a = a.reshape(NDC, 128, NFB, 128).transpose(2, 1, 0, 3)
            w["%s%d" % (nm, f)] = np.ascontiguousarray(a)
        w["wd%d" % f] = np.ascontiguousarray(np.asarray(ffn_in[f][2][0], f32))
    a = np.asarray(inp["w_in"][0], f32)
    w["w_in"] = np.ascontiguousarray(a.reshape(NDC, 128, D_IN).transpose(1, 0, 2))
    nsa_in = {"k": (inp["nsa_w1_k"], inp["nsa_w2_k"], inp["nsa_pe_k"]),
              "v": (inp["nsa_w1_v"], inp["nsa_w2_v"], inp["nsa_pe_v"])}
    for kv in ("k", "v"):
        a = np.asarray(nsa_in[kv][0][0], f32)
        w["w1" + kv] = np.ascontiguousarray(a.reshape(32, 64, 128).transpose(1, 0, 2))
        w["w2" + kv] = np.ascontiguousarray(np.asarray(nsa_in[kv][1][0], f32))
        w["pe%sT" % kv] = np.ascontiguousarray(np.asarray(nsa_in[kv][2][0], f32).T)
    w["w_a2"] = np.ascontiguousarray(np.asarray(inp["gla_w_a2"][0], f32))
    w["b_a"] = np.ascontiguousarray(np.asarray(inp["gla_b_a"][0], f32).reshape(4, 64).T)
    w["gla_g"] = bc(inp["gla_norm"][0])
    w["w_out"] = np.ascontiguousarray(np.asarray(inp["w_out"][0], f32))
    return w


_CACHE = {}


def kernel(**inputs):
    x = np.asarray(inputs["x"], np.float32)
    B, T, D = x.shape
    DFF = inputs["ffn1_w_gate"].shape[-1]
    nc = bass.Bass("TRN2", target_bir_lowering=False)
    stats, cs = build(nc, T, D, DFF)
    w = prep_weights(inputs, T, D, DFF)
    for k, v in cs.items():
        w["c_" + k] = np.ascontiguousarray(v.astype(np.float32))
    in_maps = []
    for b in range(B):
        m = dict(w)
        m["x"] = np.ascontiguousarray(x[b])
        in_maps.append(m)
    res = run_bass_kernel_spmd(nc, in_maps, core_ids=list(range(B)))
    out = np.stack([np.asarray(res.results[b]["out"], np.float32) for b in range(B)], axis=0)
    return out
```
